# Optimizing a Trainium2 kernel written in Bass

```python
import math
import jax, jax.numpy as jnp
from jax import lax
import numpy as np

D_MODEL = 4096
BATCH = 1
SEQ = 16384
DEPTH = 2

N_META = 16
D_CONV = D_MODEL // 2
D_SSM = D_MODEL // 4
SSM_GROUP = 16
N_GROUPS = D_SSM // SSM_GROUP
SSM_STATE = 64
D_FF = 3 * D_MODEL
CONV_W = 3
N_IN = 3 * D_CONV + D_SSM + 2 * D_MODEL
EPS = 1e-6

kernel_name = "hybrid_conv_s5_gated_encoder"


def rmsnorm(x, g):
    x32 = x.astype(jnp.float32)
    y = x32 * lax.rsqrt(jnp.mean(x32 * x32, axis=-1, keepdims=True) + EPS)
    return (y * g.astype(jnp.float32)).astype(x.dtype)


def dwconv3(x, w):
    xp = jnp.pad(x, ((0, 0), (1, 1), (0, 0)))
    return xp[:, :-2] * w[0] + xp[:, 1:-1] * w[1] + xp[:, 2:] * w[2]


def _scan_combine(e1, e2):
    a1r, a1i, b1r, b1i = e1
    a2r, a2i, b2r, b2i = e2
    return (a2r * a1r - a2i * a1i,
            a2r * a1i + a2i * a1r,
            a2r * b1r - a2i * b1i + b2r,
            a2r * b1i + a2i * b1r + b2i)


def s5_direction(u, lam_re, lam_im, log_step, b_re, b_im, c_re, c_im, reverse):
    dt = jnp.exp(log_step)[:, None]
    mag = jnp.exp(lam_re * dt)
    abar_re = mag * jnp.cos(lam_im * dt)
    abar_im = mag * jnp.sin(lam_im * dt)
    nr, ni = abar_re - 1.0, abar_im
    den = lam_re * lam_re + lam_im * lam_im
    f_re = (nr * lam_re + ni * lam_im) / den
    f_im = (ni * lam_re - nr * lam_im) / den
    bb_re = f_re[..., None] * b_re - f_im[..., None] * b_im
    bb_im = f_re[..., None] * b_im + f_im[..., None] * b_re
    bu_re = jnp.einsum('blgh,gph->blgp', u, bb_re)
    bu_im = jnp.einsum('blgh,gph->blgp', u, bb_im)
    a_re = jnp.broadcast_to(abar_re, bu_re.shape)
    a_im = jnp.broadcast_to(abar_im, bu_re.shape)
    _, _, s_re, s_im = lax.associative_scan(
        _scan_combine, (a_re, a_im, bu_re, bu_im), axis=1, reverse=reverse)
    return (jnp.einsum('blgp,ghp->blgh', s_re, c_re)
            - jnp.einsum('blgp,ghp->blgh', s_im, c_im))


def setup_inputs(seed: int = 0) -> dict:
    key = jax.random.key(seed)
    ks = jax.random.split(key, 24)
    f32 = jnp.float32

    def nrm(k, shape, scale):
        return jax.random.normal(k, shape, f32) * scale

    n_idx = jnp.arange(SSM_STATE, dtype=f32)
    lam_re = -0.5 + nrm(ks[8], (DEPTH, 2, N_GROUPS, SSM_STATE), 0.01)
    lam_im = math.pi * n_idx + nrm(ks[9], (DEPTH, 2, N_GROUPS, SSM_STATE), 0.01)
    log_step = jax.random.uniform(ks[10], (DEPTH, 2, N_GROUPS), f32,
                                  math.log(1e-3), math.log(1e-1))
    return {
        "x": nrm(ks[0], (BATCH, SEQ, D_MODEL), 1.0),
        "meta_tokens": nrm(ks[1], (N_META, D_MODEL), 1.0),
        "norm_mix": 1.0 + nrm(ks[2], (DEPTH, D_MODEL), 0.02),
        "w_in": nrm(ks[3], (DEPTH, D_MODEL, N_IN), D_MODEL ** -0.5),
        "conv_a_w": nrm(ks[4], (DEPTH, CONV_W, D_CONV), CONV_W ** -0.5),
        "w_a": nrm(ks[5], (DEPTH, D_CONV, D_MODEL), D_CONV ** -0.5),
        "ssm_lambda_re": lam_re,
        "ssm_lambda_im": lam_im,
        "ssm_log_step": log_step,
        "ssm_b_re": nrm(ks[11], (DEPTH, 2, N_GROUPS, SSM_STATE, SSM_GROUP), (2 * SSM_GROUP) ** -0.5),
        "ssm_b_im": nrm(ks[12], (DEPTH, 2, N_GROUPS, SSM_STATE, SSM_GROUP), (2 * SSM_GROUP) ** -0.5),
        "ssm_c_re": nrm(ks[13], (DEPTH, 2, N_GROUPS, SSM_GROUP, SSM_STATE), 0.5),
        "ssm_c_im": nrm(ks[14], (DEPTH, 2, N_GROUPS, SSM_GROUP, SSM_STATE), 0.5),
        "ssm_d": nrm(ks[15], (DEPTH, D_SSM), 1.0),
        "w_glu": nrm(ks[16], (DEPTH, D_SSM, 2 * D_MODEL), D_SSM ** -0.5),
        "w_out": nrm(ks[17], (DEPTH, D_MODEL, D_MODEL), D_MODEL ** -0.5),
        "norm_ffn": 1.0 + nrm(ks[18], (DEPTH, D_MODEL), 0.02),
        "w_up": nrm(ks[19], (DEPTH, D_MODEL, 2 * D_FF), D_MODEL ** -0.5),
        "conv_ffn_w": nrm(ks[20], (DEPTH, CONV_W, 2 * D_FF), CONV_W ** -0.5),
        "w_down": nrm(ks[21], (DEPTH, D_FF, D_MODEL), D_FF ** -0.5),
        "norm_final": 1.0 + nrm(ks[22], (D_MODEL,), 0.02),
    }


def reference(x, meta_tokens, norm_mix, w_in, conv_a_w, w_a,
              ssm_lambda_re, ssm_lambda_im, ssm_log_step,
              ssm_b_re, ssm_b_im, ssm_c_re, ssm_c_im, ssm_d,
              w_glu, w_out, norm_ffn, w_up, conv_ffn_w, w_down, norm_final):
    bsz = x.shape[0]
    meta = jnp.broadcast_to(meta_tokens[None].astype(x.dtype), (bsz, N_META, D_MODEL))
    h_res = jnp.concatenate([meta, x], axis=1)
    L = h_res.shape[1]

    for l in range(DEPTH):
        h = rmsnorm(h_res, norm_mix[l])
        proj = h @ w_in[l]
        o = 0
        bg = proj[..., o:o + D_CONV]; o += D_CONV
        cg = proj[..., o:o + D_CONV]; o += D_CONV
        v = proj[..., o:o + D_CONV]; o += D_CONV
        u = proj[..., o:o + D_SSM]; o += D_SSM
        g_a = jax.nn.sigmoid(proj[..., o:o + D_MODEL]); o += D_MODEL
        g_b = jax.nn.sigmoid(proj[..., o:o + D_MODEL])

        y_a = (bg * dwconv3(cg * v, conv_a_w[l])) @ w_a[l]

        u32 = u.astype(jnp.float32).reshape(bsz, L, N_GROUPS, SSM_GROUP)
        lre = ssm_lambda_re[l].astype(jnp.float32)
        lim = ssm_lambda_im[l].astype(jnp.float32)
        lst = ssm_log_step[l].astype(jnp.float32)
        bre = ssm_b_re[l].astype(jnp.float32); bim = ssm_b_im[l].astype(jnp.float32)
        cre = ssm_c_re[l].astype(jnp.float32); cim = ssm_c_im[l].astype(jnp.float32)
        y_f = s5_direction(u32, lre[0], lim[0], lst[0], bre[0], bim[0], cre[0], cim[0], False)
        y_bw = s5_direction(u32, lre[1], lim[1], lst[1], bre[1], bim[1], cre[1], cim[1], True)
        s = (y_f + y_bw).reshape(bsz, L, D_SSM) + ssm_d[l].astype(jnp.float32) * u32.reshape(bsz, L, D_SSM)
        s = jax.nn.gelu(s).astype(x.dtype)
        glu = s @ w_glu[l]
        y_b = glu[..., :D_MODEL] * jax.nn.sigmoid(glu[..., D_MODEL:])

        h_res = h_res + (g_a * y_a + g_b * y_b) @ w_out[l]

        h = rmsnorm(h_res, norm_ffn[l])
        up = dwconv3(h @ w_up[l], conv_ffn_w[l])
        act = jax.nn.gelu(up[..., :D_FF]) * up[..., D_FF:]
        h_res = h_res + act @ w_down[l]

    out = rmsnorm(h_res, norm_final)
    return out[:, N_META:]
```

```python
import math
from contextlib import ExitStack

import numpy as np
import concourse.bass as bass
import concourse.mybir as mybir
from concourse.bass_utils import run_bass_kernel_spmd

F32 = mybir.dt.float32
BF16 = mybir.dt.bfloat16
AF = mybir.ActivationFunctionType
ALU = mybir.AluOpType
NCORES = 8
EPS = 1e-6
MAGIC = 12582912.0
TWO_PI = 2.0 * math.pi
N_META = 16


class Cfg:
    def __init__(s, D, T, N, depth=2, wdepth=2, kmax=32, semlim=30000):
        s.D, s.T, s.N, s.depth, s.wdepth = D, T, N, depth, wdepth
        s.kmax, s.semlim = kmax, semlim
        s.DT = D // 128
        s.DC = D // 2
        s.DCT = s.DC // 128
        s.DS = D // 4
        s.DST = s.DS // 128
        s.NG = s.DS // 16
        s.NGP = s.NG // 2
        s.NQ = s.NG
        s.DF = 3 * D
        s.DFT = s.DF // 128
        s.NT = T // N
        assert T % N == 0 and N + 2 <= 512
        s.KD = min(s.DT, kmax)
        s.KF = min(s.DFT, kmax)
        s.nkcF = s.DFT // s.KF
        assert s.DT <= kmax and s.DFT % s.KF == 0


def _units(W, order, KTU):
    K, C = W.shape
    nkc = K // (128 * KTU)
    W5 = W.reshape(nkc, KTU, 128, C // 128, 128).transpose(3, 0, 2, 1, 4)
    W5 = W5[np.asarray(order)]
    nu = len(order) * nkc
    out = np.ascontiguousarray(W5).reshape(nu, 128, KTU * 128)
    pad = (-nu) % NCORES
    if pad:
        out = np.concatenate([out, np.zeros((pad, 128, KTU * 128), np.float32)], 0)
    return out


def _pt(v, nt):
    return np.ascontiguousarray(v.reshape(nt, 128).T)


def weight_specs(c):
    def padn(n):
        return n + ((-n) % NCORES)
    return {
        "win": (padn(c.DST + 3 * c.DCT + 2 * c.DT), c.KD),
        "wa": (padn(c.DT), c.DCT),
        "wglu": (padn(2 * c.DT), c.DST),
        "wout": (padn(c.DT), c.KD),
        "wup": (padn(2 * c.DFT), c.KD),
        "wdown": (padn(c.DT * c.nkcF), c.KF),
    }


def host_prepare(c, inp):
    D, T = c.D, c.T
    L = NCORES * T
    full = np.concatenate([inp["meta_tokens"].astype(np.float32), inp["x"][0]], 0)
    assert full.shape[0] == L
    shared = {}
    per_core = [dict() for _ in range(NCORES)]
    for l in range(c.depth):
        w_in = inp["w_in"][l]
        o_bg, o_cg, o_v = 0, c.DCT, 2 * c.DCT
        o_u = 3 * c.DCT
        o_ga = o_u + c.DST
        o_gb = o_ga + c.DT
        order = list(range(o_u, o_u + c.DST))
        for j in range(c.DCT):
            order += [o_cg + j, o_v + j, o_bg + j]
        order += list(range(o_ga, o_ga + c.DT)) + list(range(o_gb, o_gb + c.DT))
        ws = {
            "win": _units(w_in, order, c.KD),
            "wa": _units(inp["w_a"][l], list(range(c.DT)), c.DCT),
            "wglu": _units(inp["w_glu"][l], list(range(2 * c.DT)), c.DST),
            "wout": _units(inp["w_out"][l], list(range(c.DT)), c.KD),
            "wup": _units(inp["w_up"][l], [x for i in range(c.DFT) for x in (i, c.DFT + i)], c.KD),
            "wdown": _units(inp["w_down"][l], list(range(c.DT)), c.KF),
        }
        for k, arr in ws.items():
            per = arr.shape[0] // NCORES
            for r in range(NCORES):
                per_core[r][f"{k}{l}"] = np.ascontiguousarray(arr[r * per:(r + 1) * per]).reshape(per * 128, -1)
        sm = [_pt(inp["norm_mix"][l], c.DT), _pt(inp["norm_ffn"][l], c.DT), _pt(inp["ssm_d"][l], c.DST)]
        ca = inp["conv_a_w"][l]
        sm += [_pt(ca[k], c.DCT) for k in range(3)]
        cf = inp["conv_ffn_w"][l]
        sm += [_pt(cf[k][:c.DF], c.DFT) for k in range(3)] + [_pt(cf[k][c.DF:], c.DFT) for k in range(3)]
        if l == c.depth - 1:
            sm.append(_pt(inp["norm_final"], c.DT))
        shared[f"small{l}"] = np.ascontiguousarray(np.concatenate(sm, 1))

        def gq(a):
            return a.reshape(2, c.NGP, 2, 64).transpose(2, 3, 0, 1).reshape(128, c.NQ)
        lre = gq(inp["ssm_lambda_re"][l])
        lim = gq(inp["ssm_lambda_im"][l])
        lst = gq(np.broadcast_to(inp["ssm_log_step"][l][:, :, None], (2, c.NG, 64)))
        shared[f"ssc{l}"] = np.ascontiguousarray(np.concatenate([lre, lim, lst], 1).astype(np.float32))
        bt = np.zeros((2, 128, c.NQ, 128), np.float32)
        cd = np.zeros((2, 128, c.NQ, 128), np.float32)
        for ri, (bsrc, csrc) in enumerate(((inp["ssm_b_re"][l], inp["ssm_c_re"][l]),
                                           (inp["ssm_b_im"][l], inp["ssm_c_im"][l]))):
            for d in range(2):
                for gp in range(c.NGP):
                    q = d * c.NGP + gp
                    r = gp % 4
                    for g2 in range(2):
                        g = 2 * gp + g2
                        rows = slice(r * 32 + g2 * 16, r * 32 + g2 * 16 + 16)
                        cols = slice(g2 * 64, g2 * 64 + 64)
                        bt[ri, rows, q, cols] = bsrc[d, g].T
                        cd[ri, cols, q, rows] = csrc[d, g].T
        shared[f"sbt{l}"] = bt.reshape(2 * 128, c.NQ * 128)
        shared[f"scd{l}"] = cd.reshape(2 * 128, c.NQ * 128)
    iota = np.broadcast_to(np.arange(T, dtype=np.float32)[None], (128, T))
    shared["iota"] = np.ascontiguousarray(iota)
    for r in range(NCORES):
        xt = np.ascontiguousarray(full[r * T:(r + 1) * T].T)
        per_core[r]["xin"] = xt
        halo = np.zeros((128, 2, c.DT), np.float32)
        if r > 0:
            halo[:, 0, :] = _pt(full[r * T - 1], c.DT)
        if r < NCORES - 1:
            halo[:, 1, :] = _pt(full[(r + 1) * T], c.DT)
        per_core[r]["halo0"] = halo.reshape(128, 2 * c.DT)
        cm = np.zeros((128, 32), np.float32)
        for j in range(NCORES):
            cm[:, j] = 1.0 if j < r else 0.0
            cm[:, 8 + j] = 1.0 if j > r else 0.0
            cm[:, 16 + j] = 1.0 if j == r - 1 else 0.0
            cm[:, 24 + j] = 1.0 if j == r + 1 else 0.0
        per_core[r]["cmask"] = cm
        per_core[r].update(shared)
    return per_core


SEMLIM = [30000]


class SemCtr:
    def __init__(s, K, name, step):
        s.K, s.name, s.step = K, name, step
        s.cur, s.val, s.n = None, 0, 0

    def next(s):
        if s.cur is None or s.val + s.step > SEMLIM[0]:
            s.cur = s.K.new_sem(f"{s.name}_{s.n}")
            s.n += 1
            s.val = 0
        s.val += s.step
        return (s.cur, s.val)


class Buf:
    __slots__ = ("w", "r", "name")

    def __init__(s, name=""):
        s.w, s.r, s.name = [], [], name


class K:
    ENG = ("pe", "act", "dve", "pool", "sp")

    def __init__(s, nc, es):
        s.nc, s.es = nc, es
        s.prog = {e: [] for e in s.ENG}
        s.ctr = {e: SemCtr(s, e, 1) for e in s.ENG}
        s.waited = {e: {} for e in s.ENG}
        s.nsem = 0

    def new_sem(s, name):
        s.nsem += 1
        return s.es.enter_context(s.nc.semaphore(name))

    def sb(s, name, shape, dt):
        return s.es.enter_context(s.nc.sbuf_tensor("t_" + name, shape, dt))

    def ps(s, name, shape, dt=F32):
        return s.es.enter_context(s.nc.psum_tensor(name, shape, dt))

    def _waits(s, eng, reads, writes):
        need = {}
        toks = []
        for b in reads:
            toks += b.w
        for b in writes:
            toks += b.w + b.r
        for (sem, val, src) in toks:
            if eng == "pe" and src == "pe":
                continue
            key = id(sem)
            if s.waited[eng].get(key, 0) >= val:
                continue
            if key not in need or need[key][1] < val:
                need[key] = (sem, val)
        for key, (sem, val) in need.items():
            s.waited[eng][key] = val
        return list(need.values())

    def _commit(s, tok, reads, writes):
        for b in reads:
            b.r = [t for t in b.r if not (t[0] is tok[0])] + [tok]
        for b in writes:
            b.w = [tok]
            b.r = []

    def op(s, eng, fn, reads=(), writes=()):
        waits = s._waits(eng, reads, writes)
        sem, val = s.ctr[eng].next()
        tok = (sem, val, eng)
        s.prog[eng].append((waits, fn, sem, 1))
        s._commit(tok, reads, writes)
        return tok

    def dma(s, eng, fn, slot_ctr, reads=(), writes=()):
        waits = s._waits(eng, reads, writes)
        sem, val = slot_ctr.next()
        tok = (sem, val, "dma")
        s.prog[eng].append((waits, fn, sem, 16))
        s._commit(tok, reads, writes)
        return tok

    def wait_all(s, eng, bufs):
        waits = s._waits(eng, [], bufs)
        s.prog[eng].append((waits, None, None, 0))

    def replay(s):
        nc = s.nc
        handles = {"pe": "tensor", "act": "scalar", "dve": "vector", "pool": "gpsimd", "sp": "sync"}
        with nc.Block() as block:
            for e in s.ENG:
                prog = s.prog[e]

                def body(eng, prog=prog):
                    for waits, fn, sem, inc in prog:
                        for (ws, wv) in waits:
                            eng.wait_ge(ws, wv)
                        if fn is not None:
                            ins = fn(eng)
                            if inc == 1:
                                ins.then_inc(sem, 1)
                            elif inc == 16:
                                ins.then_inc(sem, 16)
                            elif inc == -1:
                                ins.then_inc(sem)
                getattr(block, handles[e])(body)


def build_program(c):
    nc = bass.Bass("TRN2", target_bir_lowering=False)
    SEMLIM[0] = c.semlim
    es = ExitStack()
    k = K(nc, es)
    D, T, N, DT, DCT, DST, DFT, NQ, NGP = c.D, c.T, c.N, c.DT, c.DCT, c.DST, c.DFT, c.NQ, c.NGP
    NT = c.NT
    N2 = N + 2
    wspec = weight_specs(c)

    def din(name, shape, dt=F32):
        return nc.dram_tensor(name, list(shape), dt, kind="ExternalInput")

    xin = din("xin", [D, T])
    halo0 = din("halo0", [128, 2 * DT])
    cmask_d = din("cmask", [128, 32])
    iota_d = din("iota", [128, T])
    nsmall = [2 * DT + DST + 3 * DCT + 6 * DFT + (DT if l == c.depth - 1 else 0) for l in range(c.depth)]
    small_d = [din(f"small{l}", [128, nsmall[l]]) for l in range(c.depth)]
    ssc_d = [din(f"ssc{l}", [128, 3 * NQ]) for l in range(c.depth)]
    sbt_d = [din(f"sbt{l}", [256, NQ * 128]) for l in range(c.depth)]
    scd_d = [din(f"scd{l}", [256, NQ * 128]) for l in range(c.depth)]
    wsh, wsrc, wrep = {}, {}, {}
    for l in range(c.depth):
        for nm, (nu, ktu) in wspec.items():
            per = nu // NCORES
            wsh[nm, l] = din(f"{nm}{l}", [per * 128, ktu * 128])
            wsrc[nm, l] = nc.dram_tensor(f"{nm}{l}_src", [per * 128, ktu * 128], BF16)
            wrep[nm, l] = nc.dram_tensor(f"{nm}{l}_rep", [nu * 128, ktu * 128], BF16)
    outd = nc.dram_tensor("out", [D, T], F32, kind="ExternalOutput")
    xs = {}
    for l in range(c.depth):
        xs[l, "a"] = nc.dram_tensor(f"x1_{l}", [D, T], F32)
        xs[l, "b"] = nc.dram_tensor(f"x2_{l}", [D, T], F32)
    e_src = nc.dram_tensor("e_src", [128, 2 * NQ], F32)
    e_all = nc.dram_tensor("e_all", [NCORES * 128, 2 * NQ], F32)
    h_src = nc.dram_tensor("h_src", [128, 2 * DT], F32)
    h_all = nc.dram_tensor("h_all", [NCORES * 128, 2 * DT], F32)

    xbuf = {}

    def xb(key, j):
        return xbuf.setdefault((key, j), Buf(f"x{key}{j}"))
    wrep_buf = {kk: Buf(f"w{kk}") for kk in wrep}

    WU = max(ktu for (_, ktu) in wspec.values()) * 128
    wslot_b = [Buf(f"wslot{i}") for i in range(c.wdepth)]
    wslot_c = [SemCtr(k, f"wsl{i}", 16) for i in range(c.wdepth)]
    NXS = 2
    xst = [k.sb(f"xst{i}", [128, N2], F32) for i in range(NXS)]
    xst_b = [Buf(f"xst{i}") for i in range(NXS)]
    xst_c = [SemCtr(k, f"xsl{i}", 16) for i in range(NXS)]
    NOS = 2
    ost = [k.sb(f"ost{i}", [128, N], F32) for i in range(NOS)]
    ost_b = [Buf(f"ost{i}") for i in range(NOS)]
    ost_c = [SemCtr(k, f"osl{i}", 16) for i in range(NOS)]
    class _MC(dict):
        def __missing__(self, key):
            self[key] = SemCtr(k, f"m{key}", 16)
            return self[key]
    mc = _MC()

    hT_b = [Buf(f"hT{i}") for i in range(DT)]
    Rt = k.sb("Rt", [128, N2], F32)
    Rt_b = Buf("Rt")
    sq = [k.sb(f"sq{i}", [128, N2], F32) for i in range(2)]
    sq_b = [Buf(), Buf()]
    US = k.sb("US", [128, DST, T], BF16)
    US_b = [Buf(f"US{i}") for i in range(DST)]
    ones = k.sb("ones", [128, 128], F32)
    ones_b = Buf("ones")
    small = [k.sb(f"small{l}", [128, nsmall[l]], F32) for l in range(c.depth)]
    small_b = [Buf() for _ in range(c.depth)]
    cmask = k.sb("cmaskt", [128, 32], F32)
    cmask_b = Buf()
    halo = [k.sb(f"halo{i}", [128, 2, DT], F32) for i in range(2)]
    halo_b = [Buf(), Buf()]
    hb = k.sb("hbnd", [128, 2, DT], F32)
    hb_b = Buf()
    hg = k.sb("hgath", [128, NCORES, 2 * DT], F32)
    hg_b = Buf()
    P2W = 6 * T * 4 + 2 * T * 2 + T * 4 + 4 * 8 * 2 * 128 * 2
    BASEW = DT * N2 * 2
    P3W = BASEW + (DCT * N + DT * N) * 2 + 8 * N2 * 4
    P4W = BASEW + DFT * N * 2 + 8 * N2 * 4
    BIGW = max(P2W, P3W, P4W)
    big = k.sb("big", [128, BIGW], mybir.dt.uint8)

    def carve(off, shape, dt):
        size = int(np.prod(shape)) * (2 if dt == BF16 else 4)
        assert off + size <= BIGW, (off, size, BIGW)
        assert off % 4 == 0, off
        ap = big[:, off:off + size].bitcast(dt)
        if len(shape) == 2:
            ap = ap.rearrange("p (a b) -> p a b", b=shape[1])
        if len(shape) == 3:
            ap = ap.rearrange("p (a b c) -> p a b c", b=shape[1], c=shape[2])
        return ap, off + size

    _o = 0
    wslot = [k.sb(f"wslot{i}", [128, WU], BF16) for i in range(c.wdepth)]
    ldst_s = k.sb("ldst", [128, 1024], F32)
    ldst2_s = k.sb("ldst2", [128, 1024], F32)
    iota_s = k.sb("iota", [128, T], F32)
    hT, _o = carve(_o, [DT, N2], BF16)
    PH_OFF = _o
    region = {"bufs": []}

    def enter_phase(new_bufs):
        toks = []
        for b in region["bufs"]:
            toks += b.w + b.r
        for nb in new_bufs:
            nb.r = nb.r + toks
        region["bufs"] = list(new_bufs)

    NPS = 8
    psb = [k.ps(f"ps{i}", [128, 512]) for i in range(NPS)]
    psb_b = [Buf(f"ps{i}") for i in range(NPS)]
    st = {"ps": 0, "xs": 0, "os": 0, "ws": 0, "sq": 0}

    def nxt(key, n):
        i = st[key]
        st[key] = (i + 1) % n
        return i

    def small_off(l):
        o = {}
        p = 0
        for nm, n in (("gmix", DT), ("gffn", DT), ("dskip", DST), ("ca0", DCT), ("ca1", DCT), ("ca2", DCT),
                      ("cg0", DFT), ("cg1", DFT), ("cg2", DFT), ("cv0", DFT), ("cv1", DFT), ("cv2", DFT)):
            o[nm] = p
            p += n
        if l == c.depth - 1:
            o["gfin"] = p
        return o

    k.op("dve", lambda e: e.memset(ones[:], 1.0), writes=[ones_b])
    k.dma("sp", lambda e: e.dma_start(out=cmask[:], in_=cmask_d[:, :]), mc["cmask"], writes=[cmask_b])
    k.dma("sp", lambda e: e.dma_start(out=halo[0][:].rearrange("p a b -> p (a b)"), in_=halo0[:, :]), mc["halo0"],
          writes=[halo_b[0]])
    for l in range(c.depth):
        k.dma("sp", lambda e, l=l: e.dma_start(out=small[l][:], in_=small_d[l][:, :]), mc[f"small{l}"], writes=[small_b[l]])

    cast_c = SemCtr(k, "cast", 16)
    cc_ctr = SemCtr(k, "cc", 1)
    wsrc_b = {kk: Buf() for kk in wsrc}

    def weight_prep(l):
        for nm in ("win", "wa", "wglu", "wout", "wup", "wdown"):
            nu, ktu = wspec[nm]
            rows = (nu // NCORES) * 128
            cols = ktu * 128
            src = wsh[nm, l]
            dst = wsrc[nm, l]
            sub = 1
            while cols // sub > 2048 or cols % sub:
                sub += 1
            sv = src.ap().rearrange("r (s e) -> (r s) e", s=sub) if sub > 1 else src.ap()
            dv = dst.ap().rearrange("r (s e) -> (r s) e", s=sub) if sub > 1 else dst.ap()
            R = rows * sub
            CH = 4096
            for r0 in range(0, R, CH):
                r1 = min(R, r0 + CH)
                tk = k.dma("pool", lambda e, sv=sv, dv=dv, r0=r0, r1=r1: e.dma_start(out=dv[r0:r1, :], in_=sv[r0:r1, :]),
                           cast_c)
                wsrc_b[nm, l].w = [tk]
            waits = k._waits("pool", [wsrc_b[nm, l]], [wrep_buf[nm, l]])
            sem, val = cc_ctr.next()

            def fn(e, nm=nm, l=l):
                return e.collective_compute("AllGather", ALU.bypass, replica_groups=[list(range(NCORES))],
                                            ins=[wsrc[nm, l].ap().opt()], outs=[wrep[nm, l].ap().opt()])
            k.prog["pool"].append((waits, fn, sem, -1))
            wrep_buf[nm, l].w = [(sem, val, "cc")]
            wrep_buf[nm, l].r = []

    def load_units(units):
        issued = []

        def issue(u):
            nm, l, ui = u
            ktu = wspec[nm][1]
            si = nxt("ws", c.wdepth)
            src = wrep[nm, l]
            k.dma("sp", lambda e, si=si, src=src, ui=ui, ktu=ktu: e.dma_start(
                out=wslot[si][:, 0:ktu * 128], in_=src[ui * 128:(ui + 1) * 128, :]),
                wslot_c[si], reads=[wrep_buf[nm, l]], writes=[wslot_b[si]])
            return si
        n = len(units)
        pre = min(c.wdepth - 1, n)
        for i in range(pre):
            issued.append(issue(units[i]))
        for i in range(n):
            yield issued[i]
            if i + pre < n:
                issued.append(issue(units[i + pre]))

    def mm_group(si, ktu, rhs_fn, rhs_bufs, ncols, extra_acc=None):
        pi = nxt("ps", NPS)
        for kt in range(ktu):
            k.op("pe", lambda e, pi=pi, si=si, kt=kt: e.matmul(
                psb[pi][:, 0:ncols], lhsT=wslot[si][:, kt * 128:(kt + 1) * 128], rhs=rhs_fn(kt),
                start=(kt == 0), stop=(kt == ktu - 1)),
                reads=[wslot_b[si]] + rhs_bufs, writes=[psb_b[pi]])
        return pi

    def mm_multi(sis, ktu, rhs_fn, rhs_bufs, ncols):
        pi = nxt("ps", NPS)
        tot = len(sis) * ktu
        idx = 0
        for ci, si in enumerate(sis):
            for kt in range(ktu):
                k.op("pe", lambda e, pi=pi, si=si, kt=kt, ci=ci, idx=idx: e.matmul(
                    psb[pi][:, 0:ncols], lhsT=wslot[si][:, kt * 128:(kt + 1) * 128], rhs=rhs_fn(ci * ktu + kt),
                    start=(idx == 0), stop=(idx == tot - 1)),
                    reads=[wslot_b[si]] + rhs_bufs, writes=[psb_b[pi]])
                idx += 1
        return pi

    def load_x(src_t, key, hl_t, hl_b, j, i):
        si = nxt("xs", NXS)
        lo = j * N - 1
        hi = j * N + N + 1
        c0 = 0
        if lo < 0:
            lo, c0 = 0, 1
        if hi > T:
            hi = T
        w = hi - lo
        rb = [xb(key, jj) for jj in (j - 1, j, j + 1) if 0 <= jj < NT]
        k.dma("sp", lambda e: e.dma_start(out=xst[si][:, c0:c0 + w], in_=src_t[i * 128:(i + 1) * 128, lo:hi]),
              xst_c[si], reads=rb, writes=[xst_b[si]])
        if j == 0:
            k.op("act", lambda e: e.copy(out=xst[si][:, 0:1], in_=hl_t[:, 0, i:i + 1]),
                 reads=[hl_b], writes=[xst_b[si]])
        if j == NT - 1:
            k.op("act", lambda e: e.copy(out=xst[si][:, N + 1:N + 2], in_=hl_t[:, 1, i:i + 1]),
                 reads=[hl_b], writes=[xst_b[si]])
        return si

    def norm_tile(src_t, key, hl_t, hl_b, j, gain_ap_fn, gain_b):
        pi = nxt("ps", NPS)
        for i in range(DT):
            si = load_x(src_t, key, hl_t, hl_b, j, i)
            qi = nxt("sq", 2)
            k.op("act", lambda e, si=si, qi=qi: e.activation(out=sq[qi][:], in_=xst[si][:], func=AF.Square),
                 reads=[xst_b[si]], writes=[sq_b[qi]])
            k.op("pe", lambda e, qi=qi, i=i: e.matmul(psb[pi][:, 0:N2], lhsT=ones[:], rhs=sq[qi][:],
                                                      start=(i == 0), stop=(i == DT - 1)),
                 reads=[ones_b, sq_b[qi]], writes=[psb_b[pi]])
        k.op("dve", lambda e: e.tensor_scalar(out=Rt[:], in0=psb[pi][:, 0:N2], scalar1=1.0 / D, scalar2=EPS,
                                              op0=ALU.mult, op1=ALU.add), reads=[psb_b[pi]], writes=[Rt_b])
        k.op("act", lambda e: e.activation(out=Rt[:], in_=Rt[:], func=AF.Sqrt), reads=[Rt_b], writes=[Rt_b])
        k.op("dve", lambda e: e.reciprocal(out=Rt[:], in_=Rt[:]), reads=[Rt_b], writes=[Rt_b])
        for i in range(DT):
            si = load_x(src_t, key, hl_t, hl_b, j, i)
            k.op("dve", lambda e, si=si, i=i: e.scalar_tensor_tensor(
                out=hT[:, i, :], in0=xst[si][:], scalar=gain_ap_fn(i), in1=Rt[:], op0=ALU.mult, op1=ALU.mult),
                reads=[xst_b[si], Rt_b, gain_b], writes=[hT_b[i]])

    def store_x(dst_t, key, j, i, oi):
        k.dma("sp", lambda e: e.dma_start(out=dst_t[i * 128:(i + 1) * 128, j * N:(j + 1) * N], in_=ost[oi][:]),
              ost_c[oi], reads=[ost_b[oi]], writes=[])
        tok = ost_b[oi].r[-1]
        xb(key, j).w.append(tok)

    def gelu_ops(x_ap, xb_, t1, t1b, t2, t2b, out_ap, outb):
        k.op("dve", lambda e: e.tensor_tensor(out=t1, in0=x_ap, in1=x_ap, op=ALU.mult), reads=[xb_], writes=[t1b])
        k.op("dve", lambda e: e.tensor_scalar(out=t1, in0=t1, scalar1=0.044715, scalar2=1.0, op0=ALU.mult,
                                              op1=ALU.add), reads=[t1b], writes=[t1b])
        k.op("dve", lambda e: e.tensor_tensor(out=t1, in0=t1, in1=x_ap, op=ALU.mult), reads=[t1b, xb_], writes=[t1b])
        k.op("act", lambda e: e.activation(out=t2, in_=t1, func=AF.Sigmoid, scale=1.5957691216057308),
             reads=[t1b], writes=[t2b])
        k.op("dve", lambda e: e.tensor_tensor(out=out_ap, in0=t2, in1=x_ap, op=ALU.mult), reads=[t2b, xb_],
             writes=[outb])

    def halo_exchange(dst_halo, dst_b):
        tk = k.dma("sp", lambda e: e.dma_start(out=h_src[:, :], in_=hb[:].rearrange("p a b -> p (a b)")), mc["hb"],
                   reads=[hb_b])
        hsb = Buf()
        hsb.w = [tk]
        hab = Buf()
        waits = k._waits("pool", [hsb], [hab])
        sem, val = cc_ctr.next()
        k.prog["pool"].append((waits, lambda e: e.collective_compute(
            "AllGather", ALU.bypass, replica_groups=[list(range(NCORES))],
            ins=[h_src.ap().opt()], outs=[h_all.ap().opt()]), sem, -1))
        hab.w = [(sem, val, "cc")]
        k.dma("sp", lambda e: e.dma_start(out=hg[:], in_=h_all.ap().rearrange("(r p) f -> p r f", p=128)), mc["hg"],
              reads=[hab], writes=[hg_b])
        for side, (moff, a) in enumerate(((16, 1), (24, 0))):
            for r in range(NCORES):
                if r == 0:
                    k.op("dve", lambda e, side=side, moff=moff, a=a, r=r: e.tensor_scalar(
                        out=dst_halo[:, side, :], in0=hg[:, r, a * DT:(a + 1) * DT],
                        scalar1=cmask[:, moff + r:moff + r + 1], scalar2=None, op0=ALU.mult),
                        reads=[hg_b, cmask_b], writes=[dst_b])
                else:
                    k.op("dve", lambda e, side=side, moff=moff, a=a, r=r: e.scalar_tensor_tensor(
                        out=dst_halo[:, side, :], in0=hg[:, r, a * DT:(a + 1) * DT],
                        scalar=cmask[:, moff + r:moff + r + 1], in1=dst_halo[:, side, :],
                        op0=ALU.mult, op1=ALU.add), reads=[hg_b, cmask_b, dst_b], writes=[dst_b])

    sT = [k.sb(f"s5_{nm}", [128, NQ], F32) for nm in range(28)]
    sT_b = [Buf() for _ in range(28)]
    (I_LRE, I_LIM, I_DT, I_LDRE, I_THR, I_MAG, I_C1, I_S1, I_AR, I_AI, I_FR, I_FI, I_T1, I_T2, I_T3, I_T4,
     I_CT1, I_ST1, I_ATR, I_ATI, I_GER, I_GEI, I_ER, I_EI, I_FRE, I_FIM, I_INR, I_INI) = range(28)
    ssc = k.sb("ssc", [128, 3 * NQ], F32)
    ssc_b = Buf()
    eg = k.sb("egath", [128, NCORES, 2 * NQ], F32)
    eg_b = Buf()
    est = k.sb("estage", [128, 2 * NQ], F32)
    est_b = Buf()
    BT_b = [Buf(), Buf()]
    CT_b = [Buf(), Buf()]
    ldst_b = Buf()
    ldst2_b = Buf()
    iota_b = Buf()
    s5v = {}

    def dv(fn, reads, writes):
        return k.op("dve", fn, reads=reads, writes=writes)

    def tt(out, a, b, op, ib, ob):
        dv(lambda e: e.tensor_tensor(out=out, in0=a, in1=b, op=op), ib, ob)

    def S(i, lo=0, hi=None):
        return sT[i][:, lo:(NQ if hi is None else hi)]

    def cmul(o_r, o_i, a_r, a_i, b_r, b_i, lo=0, hi=None):
        tt(S(I_T3, lo, hi), S(a_r, lo, hi), S(b_r, lo, hi), ALU.mult, [sT_b[a_r], sT_b[b_r]], [sT_b[I_T3]])
        tt(S(I_T4, lo, hi), S(a_i, lo, hi), S(b_i, lo, hi), ALU.mult, [sT_b[a_i], sT_b[b_i]], [sT_b[I_T4]])
        tt(S(o_r, lo, hi), S(I_T3, lo, hi), S(I_T4, lo, hi), ALU.subtract, [sT_b[I_T3], sT_b[I_T4]], [sT_b[o_r]])
        tt(S(I_T3, lo, hi), S(a_r, lo, hi), S(b_i, lo, hi), ALU.mult, [sT_b[a_r], sT_b[b_i]], [sT_b[I_T3]])
        tt(S(I_T4, lo, hi), S(a_i, lo, hi), S(b_r, lo, hi), ALU.mult, [sT_b[a_i], sT_b[b_r]], [sT_b[I_T4]])
        tt(S(o_i, lo, hi), S(I_T3, lo, hi), S(I_T4, lo, hi), ALU.add, [sT_b[I_T3], sT_b[I_T4]], [sT_b[o_i]])

    def cs_small(o_c, o_s, mult):
        dv(lambda e: e.tensor_scalar(out=S(I_T1), in0=S(I_THR), scalar1=mult / TWO_PI, scalar2=MAGIC, op0=ALU.mult,
                                     op1=ALU.add), [sT_b[I_THR]], [sT_b[I_T1]])
        dv(lambda e: e.tensor_scalar(out=S(I_T1), in0=S(I_T1), scalar1=MAGIC, scalar2=-TWO_PI, op0=ALU.subtract,
                                     op1=ALU.mult), [sT_b[I_T1]], [sT_b[I_T1]])
        dv(lambda e: e.scalar_tensor_tensor(out=S(I_T1), in0=S(I_THR), scalar=float(mult), in1=S(I_T1), op0=ALU.mult,
                                            op1=ALU.add), [sT_b[I_THR], sT_b[I_T1]], [sT_b[I_T1]])
        dv(lambda e: e.tensor_scalar(out=S(I_T1), in0=S(I_T1), scalar1=3.141592, scalar2=-3.141592, op0=ALU.min,
                                     op1=ALU.max), [sT_b[I_T1]], [sT_b[I_T1]])
        k.op("act", lambda e: e.activation(out=S(o_s), in_=S(I_T1), func=AF.Sin), reads=[sT_b[I_T1]],
             writes=[sT_b[o_s]])
        dv(lambda e: e.scalar_tensor_tensor(out=S(I_T2), in0=S(I_T1), scalar=-1.0, in1=S(I_T1), op0=ALU.mult,
                                            op1=ALU.min), [sT_b[I_T1]], [sT_b[I_T2]])
        dv(lambda e: e.tensor_scalar(out=S(I_T2), in0=S(I_T2), scalar1=math.pi / 2, scalar2=None, op0=ALU.add),
           [sT_b[I_T2]], [sT_b[I_T2]])
        k.op("act", lambda e: e.activation(out=S(o_c), in_=S(I_T2), func=AF.Sin), reads=[sT_b[I_T2]],
             writes=[sT_b[o_c]])

    def s5_prep(l):
        k.dma("sp", lambda e: e.dma_start(out=ssc[:], in_=ssc_d[l][:, :]), mc["ssc"], writes=[ssc_b])
        for idx, dst in enumerate((I_LRE, I_LIM)):
            k.op("act", lambda e, idx=idx, dst=dst: e.copy(out=S(dst), in_=ssc[:, idx * NQ:(idx + 1) * NQ]),
                 reads=[ssc_b], writes=[sT_b[dst]])
        k.op("act", lambda e: e.activation(out=S(I_DT), in_=ssc[:, 2 * NQ:3 * NQ], func=AF.Exp), reads=[ssc_b],
             writes=[sT_b[I_DT]])
        tt(S(I_LDRE), S(I_LRE), S(I_DT), ALU.mult, [sT_b[I_LRE], sT_b[I_DT]], [sT_b[I_LDRE]])
        tt(S(I_THR), S(I_LIM), S(I_DT), ALU.mult, [sT_b[I_LIM], sT_b[I_DT]], [sT_b[I_THR]])
        dv(lambda e: e.tensor_scalar(out=S(I_T1), in0=S(I_THR), scalar1=1.0 / TWO_PI, scalar2=MAGIC, op0=ALU.mult,
                                     op1=ALU.add), [sT_b[I_THR]], [sT_b[I_T1]])
        dv(lambda e: e.tensor_scalar(out=S(I_T1), in0=S(I_T1), scalar1=MAGIC, scalar2=-TWO_PI, op0=ALU.subtract,
                                     op1=ALU.mult), [sT_b[I_T1]], [sT_b[I_T1]])
        tt(S(I_THR), S(I_THR), S(I_T1), ALU.add, [sT_b[I_THR], sT_b[I_T1]], [sT_b[I_THR]])
        k.op("act", lambda e: e.activation(out=S(I_MAG), in_=S(I_LDRE), func=AF.Exp), reads=[sT_b[I_LDRE]],
             writes=[sT_b[I_MAG]])
        cs_small(I_C1, I_S1, 1.0)
        tt(S(I_AR), S(I_MAG), S(I_C1), ALU.mult, [sT_b[I_MAG], sT_b[I_C1]], [sT_b[I_AR]])
        tt(S(I_AI), S(I_MAG), S(I_S1), ALU.mult, [sT_b[I_MAG], sT_b[I_S1]], [sT_b[I_AI]])
        dv(lambda e: e.tensor_scalar(out=S(I_GER), in0=S(I_AR), scalar1=-1.0, scalar2=None, op0=ALU.add),
           [sT_b[I_AR]], [sT_b[I_GER]])
        tt(S(I_T1), S(I_LRE), S(I_LRE), ALU.mult, [sT_b[I_LRE]], [sT_b[I_T1]])
        tt(S(I_T2), S(I_LIM), S(I_LIM), ALU.mult, [sT_b[I_LIM]], [sT_b[I_T2]])
        tt(S(I_T1), S(I_T1), S(I_T2), ALU.add, [sT_b[I_T1], sT_b[I_T2]], [sT_b[I_T1]])
        dv(lambda e: e.reciprocal(out=S(I_T1), in_=S(I_T1)), [sT_b[I_T1]], [sT_b[I_T1]])
        tt(S(I_T2), S(I_GER), S(I_LRE), ALU.mult, [sT_b[I_GER], sT_b[I_LRE]], [sT_b[I_T2]])
        tt(S(I_T3), S(I_AI), S(I_LIM), ALU.mult, [sT_b[I_AI], sT_b[I_LIM]], [sT_b[I_T3]])
        tt(S(I_T2), S(I_T2), S(I_T3), ALU.add, [sT_b[I_T2], sT_b[I_T3]], [sT_b[I_T2]])
        tt(S(I_FR), S(I_T2), S(I_T1), ALU.mult, [sT_b[I_T2], sT_b[I_T1]], [sT_b[I_FR]])
        tt(S(I_T2), S(I_AI), S(I_LRE), ALU.mult, [sT_b[I_AI], sT_b[I_LRE]], [sT_b[I_T2]])
        tt(S(I_T3), S(I_GER), S(I_LIM), ALU.mult, [sT_b[I_GER], sT_b[I_LIM]], [sT_b[I_T3]])
        tt(S(I_T2), S(I_T2), S(I_T3), ALU.subtract, [sT_b[I_T2], sT_b[I_T3]], [sT_b[I_T2]])
        tt(S(I_FI), S(I_T2), S(I_T1), ALU.mult, [sT_b[I_T2], sT_b[I_T1]], [sT_b[I_FI]])
        cs_small(I_CT1, I_ST1, float(T - 1))
        cs_small(I_ATR, I_ATI, float(T))
        k.op("act", lambda e: e.activation(out=S(I_T1), in_=S(I_LDRE), func=AF.Exp, scale=float(T)),
             reads=[sT_b[I_LDRE]], writes=[sT_b[I_T1]])
        tt(S(I_ATR), S(I_ATR), S(I_T1), ALU.mult, [sT_b[I_ATR], sT_b[I_T1]], [sT_b[I_ATR]])
        tt(S(I_ATI), S(I_ATI), S(I_T1), ALU.mult, [sT_b[I_ATI], sT_b[I_T1]], [sT_b[I_ATI]])

    def s5_bt(l, kk, ci):
        ldst, BT = s5v["ldst"], s5v["BT"]
        for ri in range(2):
            for d in range(2):
                q0 = d * NGP + 4 * kk
                k.dma("sp", lambda e, ri=ri, q0=q0: e.dma_start(
                    out=ldst[:, 0:512], in_=sbt_d[l][ri * 128:(ri + 1) * 128, q0 * 128:(q0 + 4) * 128]), mc["ldst"],
                    writes=[ldst_b])
                k.op("act", lambda e, ri=ri, d=d: e.copy(
                    out=BT[ci][:, d * 4:d * 4 + 4, ri, :], in_=ldst[:, 0:512].rearrange("p (q c) -> p q c", c=128)),
                    reads=[ldst_b], writes=[BT_b[ci]])

    def s5_ct(l, kk, ci):
        ldst, ldst2, CT = s5v["ldst"], s5v["ldst2"], s5v["CT"]
        for d in range(2):
            q0 = d * NGP + 4 * kk
            k.dma("sp", lambda e, q0=q0: e.dma_start(out=ldst[:, 0:512], in_=scd_d[l][0:128, q0 * 128:(q0 + 4) * 128]),
                  mc["ldst"], writes=[ldst_b])
            k.dma("sp", lambda e, q0=q0: e.dma_start(out=ldst2[:, 0:512],
                                                     in_=scd_d[l][128:256, q0 * 128:(q0 + 4) * 128]),
                  mc["ldst2"], writes=[ldst2_b])
            for r in range(4):
                q = q0 + r
                cre = ldst[:, r * 128:(r + 1) * 128]
                cim = ldst2[:, r * 128:(r + 1) * 128]
                t_a = ldst[:, 512 + r * 128:512 + (r + 1) * 128]
                t_b = ldst2[:, 512 + r * 128:512 + (r + 1) * 128]
                fr = sT[I_FR][:, q:q + 1]
                fi = sT[I_FI][:, q:q + 1]
                rb = [ldst_b, ldst2_b, sT_b[I_FR], sT_b[I_FI]]
                dv(lambda e, cim=cim, fi=fi, t_a=t_a: e.tensor_scalar(out=t_a, in0=cim, scalar1=fi, scalar2=None,
                                                                      op0=ALU.mult), rb, [ldst_b])
                dv(lambda e, cre=cre, fr=fr, t_a=t_a, d=d, r=r: e.scalar_tensor_tensor(
                    out=CT[ci][:, d * 4 + r, 0, :], in0=cre, scalar=fr, in1=t_a, op0=ALU.mult, op1=ALU.subtract),
                    rb, [CT_b[ci]])
                dv(lambda e, cim=cim, fr=fr, t_b=t_b: e.tensor_scalar(out=t_b, in0=cim, scalar1=fr, scalar2=None,
                                                                      op0=ALU.mult), rb, [ldst2_b])
                dv(lambda e, cre=cre, fi=fi, t_b=t_b, d=d, r=r: e.scalar_tensor_tensor(
                    out=CT[ci][:, d * 4 + r, 1, :], in0=cre, scalar=fi, in1=t_b, op0=ALU.mult, op1=ALU.add),
                    rb, [CT_b[ci]])

    def s5_phase(l):
        so = small_off(l)
        off = 0
        cosv, off = carve(off, [T], F32)
        sinv, off = carve(off, [T], F32)
        bpr, off = carve(off, [T], F32)
        bpi, off = carve(off, [T], F32)
        gr, off = carve(off, [T], F32)
        gi, off = carve(off, [T], F32)
        yacc, off = carve(off, [T], F32)
        hr, off = carve(off, [T], BF16)
        hi_, off = carve(off, [T], BF16)
        iota_t = iota_s[:]
        BT, CT = [], []
        for _ in range(2):
            t_, off = carve(off, [8, 2, 128], BF16)
            BT.append(t_)
        for _ in range(2):
            t_, off = carve(off, [8, 2, 128], BF16)
            CT.append(t_)
        ldst, ldst2 = ldst_s[:], ldst2_s[:]
        s5v.update(ldst=ldst, ldst2=ldst2, BT=BT, CT=CT)
        cos_b, sin_b, bpr_b, bpi_b, gr_b, gi_b, yacc_b, hr_b, hi_b = [Buf() for _ in range(9)]
        enter_phase([cos_b, sin_b, bpr_b, bpi_b, gr_b, gi_b, yacc_b, hr_b, hi_b] + BT_b + CT_b)
        if l == 0:
            k.dma("sp", lambda e: e.dma_start(out=iota_s[:], in_=iota_d[:, :]), mc["iota"], writes=[iota_b])
        s5_prep(l)
        for pass_ in (1, 2):
            for kk in range(DST):
                ci = kk % 2
                s5_bt(l, kk, ci)
                if pass_ == 2:
                    s5_ct(l, kk, ci)
                    k.op("act", lambda e, kk=kk: e.activation(
                        out=yacc, in_=US[:, kk, :], func=AF.Copy,
                        scale=small[l][:, so["dskip"] + kk:so["dskip"] + kk + 1]),
                        reads=[US_b[kk], small_b[l]], writes=[yacc_b])
                for d in range(2):
                    for r in range(4):
                        q = d * NGP + 4 * kk + r
                        thq = sT[I_THR][:, q:q + 1]
                        dv(lambda e, thq=thq: e.tensor_scalar(out=gr, in0=iota_t, scalar1=thq, scalar2=1.0 / TWO_PI,
                                                              op0=ALU.mult, op1=ALU.mult), [iota_b, sT_b[I_THR]], [gr_b])
                        dv(lambda e: e.tensor_scalar(out=gr, in0=gr, scalar1=MAGIC, scalar2=MAGIC, op0=ALU.add,
                                                     op1=ALU.subtract), [gr_b], [gr_b])
                        dv(lambda e: e.tensor_scalar(out=gr, in0=gr, scalar1=-TWO_PI, scalar2=None, op0=ALU.mult),
                           [gr_b], [gr_b])
                        dv(lambda e, thq=thq: e.scalar_tensor_tensor(out=gr, in0=iota_t, scalar=thq, in1=gr,
                                                                     op0=ALU.mult, op1=ALU.add),
                           [iota_b, sT_b[I_THR], gr_b], [gr_b])
                        dv(lambda e: e.tensor_scalar(out=gr, in0=gr, scalar1=3.141592, scalar2=-3.141592, op0=ALU.min,
                                                     op1=ALU.max), [gr_b], [gr_b])
                        k.op("act", lambda e: e.activation(out=sinv, in_=gr, func=AF.Sin), reads=[gr_b],
                             writes=[sin_b])
                        dv(lambda e: e.scalar_tensor_tensor(out=gi, in0=gr, scalar=-1.0, in1=gr, op0=ALU.mult,
                                                            op1=ALU.min), [gr_b], [gi_b])
                        dv(lambda e: e.tensor_scalar(out=gi, in0=gi, scalar1=math.pi / 2, scalar2=None, op0=ALU.add),
                           [gi_b], [gi_b])
                        k.op("act", lambda e: e.activation(out=cosv, in_=gi, func=AF.Sin), reads=[gi_b],
                             writes=[cos_b])
                        for cidx in range(NT):
                            t0 = cidx * N
                            if d == 0:
                                rhs = US[:, kk, t0:t0 + N]
                            else:
                                a0 = T - 1 - t0
                                b0 = a0 - N
                                rhs = US[:, kk, a0:(b0 if b0 >= 0 else None):-1]
                            p_r = nxt("ps", NPS)
                            k.op("pe", lambda e, p_r=p_r, rhs=rhs, d=d, r=r, ci=ci: e.matmul(
                                psb[p_r][:, 0:N], lhsT=BT[ci][:, d * 4 + r, 0, :], rhs=rhs, start=True, stop=True),
                                reads=[BT_b[ci], US_b[kk]], writes=[psb_b[p_r]])
                            p_i = nxt("ps", NPS)
                            k.op("pe", lambda e, p_i=p_i, rhs=rhs, d=d, r=r, ci=ci: e.matmul(
                                psb[p_i][:, 0:N], lhsT=BT[ci][:, d * 4 + r, 1, :], rhs=rhs, start=True, stop=True),
                                reads=[BT_b[ci], US_b[kk]], writes=[psb_b[p_i]])
                            sl = slice(t0, t0 + N)
                            dv(lambda e, p_r=p_r, sl=sl: e.tensor_tensor(out=bpr[:, sl], in0=psb[p_r][:, 0:N],
                                                                         in1=cosv[:, sl], op=ALU.mult),
                               [psb_b[p_r], cos_b], [bpr_b])
                            dv(lambda e, p_i=p_i, sl=sl: e.tensor_tensor(out=gr[:, sl], in0=psb[p_i][:, 0:N],
                                                                         in1=sinv[:, sl], op=ALU.mult),
                               [psb_b[p_i], sin_b], [gr_b])
                            dv(lambda e, sl=sl: e.tensor_tensor(out=bpr[:, sl], in0=bpr[:, sl], in1=gr[:, sl],
                                                                op=ALU.add), [bpr_b, gr_b], [bpr_b])
                            dv(lambda e, p_i=p_i, sl=sl: e.tensor_tensor(out=bpi[:, sl], in0=psb[p_i][:, 0:N],
                                                                         in1=cosv[:, sl], op=ALU.mult),
                               [psb_b[p_i], cos_b], [bpi_b])
                            dv(lambda e, p_r=p_r, sl=sl: e.tensor_tensor(out=gr[:, sl], in0=psb[p_r][:, 0:N],
                                                                         in1=sinv[:, sl], op=ALU.mult),
                               [psb_b[p_r], sin_b], [gr_b])
                            dv(lambda e, sl=sl: e.tensor_tensor(out=bpi[:, sl], in0=bpi[:, sl], in1=gr[:, sl],
                                                                op=ALU.subtract), [bpi_b, gr_b], [bpi_b])
                        magq = sT[I_MAG][:, q:q + 1].to_broadcast([128, T])
                        if pass_ == 1:
                            ini_r, ini_i, ib = 0.0, 0.0, []
                        else:
                            ini_r, ini_i = sT[I_INR][:, q:q + 1], sT[I_INI][:, q:q + 1]
                            ib = [sT_b[I_INR], sT_b[I_INI]]
                        dv(lambda e, magq=magq, ini_r=ini_r: e.tensor_tensor_scan(
                            out=gr, data0=magq, data1=bpr, initial=ini_r, op0=ALU.mult, op1=ALU.add),
                            [bpr_b, sT_b[I_MAG]] + ib, [gr_b])
                        dv(lambda e, magq=magq, ini_i=ini_i: e.tensor_tensor_scan(
                            out=gi, data0=magq, data1=bpi, initial=ini_i, op0=ALU.mult, op1=ALU.add),
                            [bpi_b, sT_b[I_MAG]] + ib, [gi_b])
                        if pass_ == 1:
                            k.op("act", lambda e, q=q: e.copy(out=sT[I_GER][:, q:q + 1], in_=gr[:, T - 1:T]),
                                 reads=[gr_b], writes=[sT_b[I_GER]])
                            k.op("act", lambda e, q=q: e.copy(out=sT[I_GEI][:, q:q + 1], in_=gi[:, T - 1:T]),
                                 reads=[gi_b], writes=[sT_b[I_GEI]])
                            continue
                        tt(bpr, gr, cosv, ALU.mult, [gr_b, cos_b], [bpr_b])
                        tt(bpi, gi, sinv, ALU.mult, [gi_b, sin_b], [bpi_b])
                        tt(hr, bpr, bpi, ALU.subtract, [bpr_b, bpi_b], [hr_b])
                        tt(bpr, gr, sinv, ALU.mult, [gr_b, sin_b], [bpr_b])
                        tt(bpi, gi, cosv, ALU.mult, [gi_b, cos_b], [bpi_b])
                        dv(lambda e: e.scalar_tensor_tensor(out=hi_, in0=bpr, scalar=-1.0, in1=bpi, op0=ALU.mult,
                                                            op1=ALU.subtract), [bpr_b, bpi_b], [hi_b])
                        for cidx in range(NT):
                            t0 = cidx * N
                            if d == 0:
                                rr, ri_ = hr[:, t0:t0 + N], hi_[:, t0:t0 + N]
                            else:
                                a0 = T - 1 - t0
                                b0 = a0 - N
                                stp = (b0 if b0 >= 0 else None)
                                rr, ri_ = hr[:, a0:stp:-1], hi_[:, a0:stp:-1]
                            p_y = nxt("ps", NPS)
                            k.op("pe", lambda e, p_y=p_y, rr=rr, d=d, r=r, ci=ci: e.matmul(
                                psb[p_y][:, 0:N], lhsT=CT[ci][:, d * 4 + r, 0, :], rhs=rr, start=True, stop=False),
                                reads=[CT_b[ci], hr_b], writes=[psb_b[p_y]])
                            k.op("pe", lambda e, p_y=p_y, ri_=ri_, d=d, r=r, ci=ci: e.matmul(
                                psb[p_y][:, 0:N], lhsT=CT[ci][:, d * 4 + r, 1, :], rhs=ri_, start=False, stop=True),
                                reads=[CT_b[ci], hi_b], writes=[psb_b[p_y]])
                            dv(lambda e, p_y=p_y, t0=t0: e.tensor_tensor(out=yacc[:, t0:t0 + N], in0=yacc[:, t0:t0 + N],
                                                                         in1=psb[p_y][:, 0:N], op=ALU.add),
                               [psb_b[p_y], yacc_b], [yacc_b])
                if pass_ == 2:
                    gelu_ops(yacc, yacc_b, bpr, bpr_b, bpi, bpi_b, US[:, kk, :], US_b[kk])
            if pass_ == 1:
                cmul(I_ER, I_EI, I_GER, I_GEI, I_CT1, I_ST1)
                k.op("act", lambda e: e.copy(out=est[:, 0:NQ], in_=S(I_ER)), reads=[sT_b[I_ER]], writes=[est_b])
                k.op("act", lambda e: e.copy(out=est[:, NQ:2 * NQ], in_=S(I_EI)), reads=[sT_b[I_EI]], writes=[est_b])
                tk = k.dma("sp", lambda e: e.dma_start(out=e_src[:, :], in_=est[:]), mc["est"], reads=[est_b])
                esb, eab = Buf(), Buf()
                esb.w = [tk]
                waits = k._waits("pool", [esb], [eab])
                sem, val = cc_ctr.next()
                k.prog["pool"].append((waits, lambda e: e.collective_compute(
                    "AllGather", ALU.bypass, replica_groups=[list(range(NCORES))],
                    ins=[e_src.ap().opt()], outs=[e_all.ap().opt()]), sem, -1))
                eab.w = [(sem, val, "cc")]
                k.dma("sp", lambda e: e.dma_start(out=eg[:], in_=e_all.ap().rearrange("(r p) f -> p r f", p=128)),
                      mc["eg"], reads=[eab], writes=[eg_b])
                for (lo, hi, order, moff) in ((0, NGP, range(NCORES), 0), (NGP, NQ, range(NCORES - 1, -1, -1), 8)):
                    dv(lambda e, lo=lo, hi=hi: e.memset(S(I_FRE, lo, hi), 0.0), [], [sT_b[I_FRE]])
                    dv(lambda e, lo=lo, hi=hi: e.memset(S(I_FIM, lo, hi), 0.0), [], [sT_b[I_FIM]])
                    for j in order:
                        cmul(I_T1, I_T2, I_ATR, I_ATI, I_FRE, I_FIM, lo, hi)
                        mj = cmask[:, moff + j:moff + j + 1]
                        for (gt, off_e, ft) in ((I_T1, 0, I_FRE), (I_T2, NQ, I_FIM)):
                            tt(S(gt, lo, hi), S(gt, lo, hi), eg[:, j, off_e + lo:off_e + hi], ALU.add,
                               [sT_b[gt], eg_b], [sT_b[gt]])
                            tt(S(gt, lo, hi), S(gt, lo, hi), S(ft, lo, hi), ALU.subtract, [sT_b[gt], sT_b[ft]],
                               [sT_b[gt]])
                            dv(lambda e, gt=gt, ft=ft, mj=mj, lo=lo, hi=hi: e.scalar_tensor_tensor(
                                out=S(ft, lo, hi), in0=S(gt, lo, hi), scalar=mj, in1=S(ft, lo, hi), op0=ALU.mult,
                                op1=ALU.add), [sT_b[gt], sT_b[ft], cmask_b], [sT_b[ft]])
                cmul(I_INR, I_INI, I_FRE, I_FIM, I_C1, I_S1)

    def phase_u(l, src_t, key, hl):
        so = small_off(l)
        enter_phase(hT_b)
        for j in range(NT):
            norm_tile(src_t, key, halo[hl], halo_b[hl], j,
                      lambda i: small[l][:, so["gmix"] + i:so["gmix"] + i + 1], small_b[l])
            units = [("win", l, u) for u in range(DST)]
            for u, si in zip(range(DST), load_units(units)):
                pi = mm_group(si, c.KD, lambda kt: hT[:, kt, 1:N + 1], hT_b, N)
                k.op("act", lambda e, pi=pi, u=u, j=j: e.copy(out=US[:, u, j * N:(j + 1) * N], in_=psb[pi][:, 0:N]),
                     reads=[psb_b[pi]], writes=[US_b[u]])

    def phase_mix(l, src_t, key, hl, dst_t, dkey):
        so = small_off(l)
        off = PH_OFF
        a_t, off = carve(off, [DCT, N], BF16)
        m_t, off = carve(off, [DT, N], BF16)
        tmp = []
        for _ in range(8):
            t_, off = carve(off, [N2], F32)
            tmp.append(t_)
        a_b = [Buf() for _ in range(DCT)]
        m_b = [Buf() for _ in range(DT)]
        tb = [Buf() for _ in range(8)]
        enter_phase(hT_b + a_b + m_b + tb)
        base = DST
        for j in range(NT):
            norm_tile(src_t, key, halo[hl], halo_b[hl], j,
                      lambda i: small[l][:, so["gmix"] + i:so["gmix"] + i + 1], small_b[l])
            units = [("win", l, base + 3 * jj + x) for jj in range(DCT) for x in range(3)]
            it = load_units(units)
            for jj in range(DCT):
                s_cg = next(it)
                p_cg = mm_group(s_cg, c.KD, lambda kt: hT[:, kt, :], hT_b, N2)
                s_v = next(it)
                p_v = mm_group(s_v, c.KD, lambda kt: hT[:, kt, :], hT_b, N2)
                k.op("act", lambda e, p_v=p_v: e.copy(out=tmp[0], in_=psb[p_v][:, 0:N2]), reads=[psb_b[p_v]],
                     writes=[tb[0]])
                dv(lambda e, p_cg=p_cg: e.tensor_tensor(out=tmp[1], in0=psb[p_cg][:, 0:N2], in1=tmp[0], op=ALU.mult),
                   [psb_b[p_cg], tb[0]], [tb[1]])
                w0, w1, w2 = [small[l][:, so[f"ca{x}"] + jj:so[f"ca{x}"] + jj + 1] for x in range(3)]
                dv(lambda e, w0=w0: e.tensor_scalar(out=tmp[2][:, 0:N], in0=tmp[1][:, 0:N], scalar1=w0, scalar2=None,
                                                    op0=ALU.mult), [tb[1], small_b[l]], [tb[2]])
                dv(lambda e, w1=w1: e.scalar_tensor_tensor(out=tmp[2][:, 0:N], in0=tmp[1][:, 1:N + 1], scalar=w1,
                                                           in1=tmp[2][:, 0:N], op0=ALU.mult, op1=ALU.add),
                   [tb[1], tb[2], small_b[l]], [tb[2]])
                dv(lambda e, w2=w2: e.scalar_tensor_tensor(out=tmp[2][:, 0:N], in0=tmp[1][:, 2:N + 2], scalar=w2,
                                                           in1=tmp[2][:, 0:N], op0=ALU.mult, op1=ALU.add),
                   [tb[1], tb[2], small_b[l]], [tb[2]])
                s_bg = next(it)
                p_bg = mm_group(s_bg, c.KD, lambda kt: hT[:, kt, 1:N + 1], hT_b, N)
                dv(lambda e, p_bg=p_bg, jj=jj: e.tensor_tensor(out=a_t[:, jj, :], in0=psb[p_bg][:, 0:N],
                                                               in1=tmp[2][:, 0:N], op=ALU.mult),
                   [psb_b[p_bg], tb[2]], [a_b[jj]])
            oga = base + 3 * DCT
            ogb = oga + DT
            units = []
            for i in range(DT):
                units += [("wa", l, i), ("win", l, oga + i), ("wglu", l, i), ("wglu", l, DT + i), ("win", l, ogb + i)]
            it = load_units(units)
            tsl = slice(j * N, (j + 1) * N)
            for i in range(DT):
                s1 = next(it)
                p_ya = mm_group(s1, DCT, lambda kt: a_t[:, kt, :], a_b, N)
                s2 = next(it)
                p_ga = mm_group(s2, c.KD, lambda kt: hT[:, kt, 1:N + 1], hT_b, N)
                k.op("act", lambda e, p_ga=p_ga: e.activation(out=tmp[3][:, 0:N], in_=psb[p_ga][:, 0:N],
                                                              func=AF.Sigmoid), reads=[psb_b[p_ga]], writes=[tb[3]])
                dv(lambda e, p_ya=p_ya: e.tensor_tensor(out=tmp[4][:, 0:N], in0=psb[p_ya][:, 0:N], in1=tmp[3][:, 0:N],
                                                        op=ALU.mult), [psb_b[p_ya], tb[3]], [tb[4]])
                s3 = next(it)
                p_l = mm_group(s3, DST, lambda kt, tsl=tsl: US[:, kt, tsl], US_b, N)
                s4 = next(it)
                p_g = mm_group(s4, DST, lambda kt, tsl=tsl: US[:, kt, tsl], US_b, N)
                k.op("act", lambda e, p_g=p_g: e.activation(out=tmp[5][:, 0:N], in_=psb[p_g][:, 0:N], func=AF.Sigmoid),
                     reads=[psb_b[p_g]], writes=[tb[5]])
                dv(lambda e, p_l=p_l: e.tensor_tensor(out=tmp[6][:, 0:N], in0=psb[p_l][:, 0:N], in1=tmp[5][:, 0:N],
                                                      op=ALU.mult), [psb_b[p_l], tb[5]], [tb[6]])
                s5 = next(it)
                p_gb = mm_group(s5, c.KD, lambda kt: hT[:, kt, 1:N + 1], hT_b, N)
                k.op("act", lambda e, p_gb=p_gb: e.activation(out=tmp[7][:, 0:N], in_=psb[p_gb][:, 0:N],
                                                              func=AF.Sigmoid), reads=[psb_b[p_gb]], writes=[tb[7]])
                tt(tmp[6][:, 0:N], tmp[6][:, 0:N], tmp[7][:, 0:N], ALU.mult, [tb[6], tb[7]], [tb[6]])
                dv(lambda e, i=i: e.tensor_tensor(out=m_t[:, i, :], in0=tmp[4][:, 0:N], in1=tmp[6][:, 0:N], op=ALU.add),
                   [tb[4], tb[6]], [m_b[i]])
            units = [("wout", l, i) for i in range(DT)]
            for i, si in zip(range(DT), load_units(units)):
                p_o = mm_group(si, c.KD, lambda kt: m_t[:, kt, :], m_b, N)
                xi = load_x(src_t, key, halo[hl], halo_b[hl], j, i)
                oi = nxt("os", NOS)
                dv(lambda e, p_o=p_o, xi=xi, oi=oi: e.tensor_tensor(out=ost[oi][:], in0=psb[p_o][:, 0:N],
                                                                    in1=xst[xi][:, 1:N + 1], op=ALU.add),
                   [psb_b[p_o], xst_b[xi]], [ost_b[oi]])
                if j == 0:
                    k.op("act", lambda e, oi=oi, i=i: e.copy(out=hb[:, 0, i:i + 1], in_=ost[oi][:, 0:1]),
                         reads=[ost_b[oi]], writes=[hb_b])
                if j == NT - 1:
                    k.op("act", lambda e, oi=oi, i=i: e.copy(out=hb[:, 1, i:i + 1], in_=ost[oi][:, N - 1:N]),
                         reads=[ost_b[oi]], writes=[hb_b])
                store_x(dst_t, dkey, j, i, oi)

    def phase_ffn(l, src_t, key, hl, dst_t, dkey):
        so = small_off(l)
        off = PH_OFF
        act_t, off = carve(off, [DFT, N], BF16)
        tmp = []
        for _ in range(8):
            t_, off = carve(off, [N2], F32)
            tmp.append(t_)
        act_b = [Buf() for _ in range(DFT)]
        tb = [Buf() for _ in range(8)]
        enter_phase(hT_b + act_b + tb)
        for j in range(NT):
            norm_tile(src_t, key, halo[hl], halo_b[hl], j,
                      lambda i: small[l][:, so["gffn"] + i:so["gffn"] + i + 1], small_b[l])
            units = [("wup", l, u) for u in range(2 * DFT)]
            it = load_units(units)
            for i in range(DFT):
                s_g = next(it)
                p_g = mm_group(s_g, c.KD, lambda kt: hT[:, kt, :], hT_b, N2)
                s_v = next(it)
                p_v = mm_group(s_v, c.KD, lambda kt: hT[:, kt, :], hT_b, N2)
                k.op("act", lambda e, p_g=p_g: e.copy(out=tmp[0], in_=psb[p_g][:, 0:N2]), reads=[psb_b[p_g]],
                     writes=[tb[0]])
                k.op("act", lambda e, p_v=p_v: e.copy(out=tmp[1], in_=psb[p_v][:, 0:N2]), reads=[psb_b[p_v]],
                     writes=[tb[1]])
                for (src_i, dst_i, pre) in ((0, 2, "cg"), (1, 3, "cv")):
                    w0, w1, w2 = [small[l][:, so[f"{pre}{x}"] + i:so[f"{pre}{x}"] + i + 1] for x in range(3)]
                    dv(lambda e, w0=w0, src_i=src_i, dst_i=dst_i: e.tensor_scalar(
                        out=tmp[dst_i][:, 0:N], in0=tmp[src_i][:, 0:N], scalar1=w0, scalar2=None, op0=ALU.mult),
                        [tb[src_i], small_b[l]], [tb[dst_i]])
                    dv(lambda e, w1=w1, src_i=src_i, dst_i=dst_i: e.scalar_tensor_tensor(
                        out=tmp[dst_i][:, 0:N], in0=tmp[src_i][:, 1:N + 1], scalar=w1, in1=tmp[dst_i][:, 0:N],
                        op0=ALU.mult, op1=ALU.add), [tb[src_i], tb[dst_i], small_b[l]], [tb[dst_i]])
                    dv(lambda e, w2=w2, src_i=src_i, dst_i=dst_i: e.scalar_tensor_tensor(
                        out=tmp[dst_i][:, 0:N], in0=tmp[src_i][:, 2:N + 2], scalar=w2, in1=tmp[dst_i][:, 0:N],
                        op0=ALU.mult, op1=ALU.add), [tb[src_i], tb[dst_i], small_b[l]], [tb[dst_i]])
                gelu_ops(tmp[2][:, 0:N], tb[2], tmp[4][:, 0:N], tb[4], tmp[5][:, 0:N], tb[5], tmp[6][:, 0:N], tb[6])
                dv(lambda e, i=i: e.tensor_tensor(out=act_t[:, i, :], in0=tmp[6][:, 0:N], in1=tmp[3][:, 0:N],
                                                  op=ALU.mult), [tb[6], tb[3]], [act_b[i]])
            units = [("wdown", l, i * c.nkcF + kc) for i in range(DT) for kc in range(c.nkcF)]
            it = load_units(units)
            for i in range(DT):
                xi = load_x(src_t, key, halo[hl], halo_b[hl], j, i)
                oi = nxt("os", NOS)
                for kc in range(c.nkcF):
                    si = next(it)
                    p_d = mm_group(si, c.KF, lambda kt, kc=kc: act_t[:, kc * c.KF + kt, :], act_b, N)
                    if kc == 0:
                        dv(lambda e, p_d=p_d, xi=xi, oi=oi: e.tensor_tensor(
                            out=ost[oi][:], in0=psb[p_d][:, 0:N], in1=xst[xi][:, 1:N + 1], op=ALU.add),
                            [psb_b[p_d], xst_b[xi]], [ost_b[oi]])
                    else:
                        dv(lambda e, p_d=p_d, oi=oi: e.tensor_tensor(
                            out=ost[oi][:], in0=psb[p_d][:, 0:N], in1=ost[oi][:], op=ALU.add),
                            [psb_b[p_d], ost_b[oi]], [ost_b[oi]])
                if j == 0:
                    k.op("act", lambda e, oi=oi, i=i: e.copy(out=hb[:, 0, i:i + 1], in_=ost[oi][:, 0:1]),
                         reads=[ost_b[oi]], writes=[hb_b])
                if j == NT - 1:
                    k.op("act", lambda e, oi=oi, i=i: e.copy(out=hb[:, 1, i:i + 1], in_=ost[oi][:, N - 1:N]),
                         reads=[ost_b[oi]], writes=[hb_b])
                store_x(dst_t, dkey, j, i, oi)

    def phase_final(l, src_t, key, hl):
        so = small_off(l)
        enter_phase(hT_b)
        fin = []
        for j in range(NT):
            norm_tile(src_t, key, halo[hl], halo_b[hl], j,
                      lambda i: small[l][:, so["gfin"] + i:so["gfin"] + i + 1], small_b[l])
            for i in range(DT):
                si = load_x(src_t, key, halo[hl], halo_b[hl], j, i)
                oi = nxt("os", NOS)
                g_ap = small[l][:, so["gfin"] + i:so["gfin"] + i + 1]
                dv(lambda e, si=si, oi=oi, g_ap=g_ap: e.scalar_tensor_tensor(
                    out=ost[oi][:], in0=xst[si][:, 1:N + 1], scalar=g_ap, in1=Rt[:, 1:N + 1], op0=ALU.mult,
                    op1=ALU.mult), [xst_b[si], Rt_b, small_b[l]], [ost_b[oi]])
                k.dma("sp", lambda e, oi=oi, i=i, j=j: e.dma_start(
                    out=outd[i * 128:(i + 1) * 128, j * N:(j + 1) * N], in_=ost[oi][:]), ost_c[oi],
                    reads=[ost_b[oi]], writes=[])
                fin.append(ost_b[oi])
        return fin

    for l in range(c.depth):
        weight_prep(l)
    src_t, key, hl = xin, "in", 0
    for l in range(c.depth):
        phase_u(l, src_t, key, hl)
        s5_phase(l)
        phase_mix(l, src_t, key, hl, xs[l, "a"], f"a{l}")
        halo_exchange(halo[1], halo_b[1])
        phase_ffn(l, xs[l, "a"], f"a{l}", 1, xs[l, "b"], f"b{l}")
        halo_exchange(halo[0], halo_b[0])
        src_t, key, hl = xs[l, "b"], f"b{l}", 0
    phase_final(c.depth - 1, src_t, key, hl)
    dbg_b = Buf()
    if getattr(c, "debug", False):
        for l in range(c.depth):
            for (ab, nm, kk_) in (("a", f"d_x1_{l}", f"a{l}"), ("b", f"d_x2_{l}", f"b{l}")):
                dd = nc.dram_tensor(nm, [D, T], F32, kind="ExternalOutput")
                tk = k.dma("sp", lambda e, dd=dd, l=l, ab=ab: e.dma_start(out=dd[:, :], in_=xs[l, ab][:, :]), mc[nm],
                           reads=[xb(kk_, j) for j in range(NT)])
                dbg_b.r.append(tk)
    k.wait_all("sp", ost_b + [dbg_b])
    k.replay()
    return nc, es


_CACHE = {}


def run(c, inputs):
    per_core = host_prepare(c, inputs)
    key = (c.D, c.T, c.N, c.kmax, c.semlim, getattr(c, "debug", False))
    if key not in _CACHE:
        _CACHE[key] = build_program(c)
    nc, es = _CACHE[key]
    res = run_bass_kernel_spmd(nc, per_core, core_ids=list(range(NCORES)))
    if getattr(c, "debug", False):
        c.dbg_out = {nm: np.concatenate([res.results[r][nm].T for r in range(NCORES)], 0)
                     for nm in res.results[0] if nm != "out"}
        c.dbg_out = {nm[2:]: v for nm, v in c.dbg_out.items()}
    outs = [res.results[r]["out"] for r in range(NCORES)]
    full = np.concatenate([o.T for o in outs], 0)
    return full[N_META:][None].astype(np.float32)


def kernel(**inputs):
    inputs = {k_: np.asarray(v) for k_, v in inputs.items()}
    c = Cfg(D=4096, T=2050, N=205)
    return run(c, inputs)
```
